# Optimizing a Trainium2 kernel written in Bass

```python
import jax, jax.numpy as jnp
from jax import lax
import numpy as np

D_MODEL = 4096
BATCH = 4
SEQ = 4096
DEPTH = 2

N_META = 16
HEAD_DIM = 64
NORM_EPS = 1e-5
NEG_INF = -1e30

RWKV_WIDTH = 3 * D_MODEL // 8
RWKV_HEADS = RWKV_WIDTH // HEAD_DIM
RWKV_DECAY_RANK = 64
RWKV_ICLR_RANK = 64
RWKV_GATE_RANK = 128
RWKV_GN_EPS = 64e-5

SWA_WIDTH = 3 * D_MODEL // 8
SWA_Q_HEADS = SWA_WIDTH // HEAD_DIM
SWA_KV_HEADS = 3
SWA_GROUP = SWA_Q_HEADS // SWA_KV_HEADS
SWA_KV_WIDTH = SWA_KV_HEADS * HEAD_DIM
SWA_WINDOW = 128
SWA_BLOCK = 128
ROPE_THETA = 500000.0
ROPE_DIMS = HEAD_DIM // 4

HGRN_WIDTH = D_MODEL - RWKV_WIDTH - SWA_WIDTH
HGRN_HEAD_DIM = 128
HGRN_HEADS = HGRN_WIDTH // HGRN_HEAD_DIM
HGRN_CHUNK = 64

PAD_FRONT = SWA_BLOCK - N_META

PEER_HEADS = 8
PEER_N_KEYS = 128
PEER_N_EXPERTS = PEER_N_KEYS ** 2
PEER_QUERY_DIM = 256
PEER_HALF_DIM = PEER_QUERY_DIM // 2
PEER_TOPK = 16
PEER_TOKEN_BLOCK = 128

RWKV_SIZES = (RWKV_WIDTH, RWKV_WIDTH, RWKV_WIDTH, RWKV_DECAY_RANK, RWKV_ICLR_RANK, RWKV_GATE_RANK)
RWKV_COLS = sum(RWKV_SIZES)
SWA_SIZES = (SWA_WIDTH, SWA_KV_WIDTH, SWA_KV_WIDTH)
HGRN_SIZES = (HGRN_WIDTH, HGRN_WIDTH, HGRN_WIDTH, HGRN_WIDTH)
IN_COLS = RWKV_COLS + sum(SWA_SIZES) + sum(HGRN_SIZES)

kernel_name = 'hybrid_rwkv7_swa_hgrn2_peer'


def _offsets(sizes):
    offs, acc = [], 0
    for s in sizes[:-1]:
        acc += s
        offs.append(acc)
    return offs


def rms_norm(x, gain, eps=NORM_EPS):
    xf = x.astype(jnp.float32)
    y = xf * lax.rsqrt(jnp.mean(xf * xf, axis=-1, keepdims=True) + eps)
    return (y * gain.astype(jnp.float32)).astype(x.dtype)


def rope_tables(length):
    pos = jnp.arange(length, dtype=jnp.float32)
    inv_freq = ROPE_THETA ** (-jnp.arange(0, ROPE_DIMS, 2, dtype=jnp.float32) / ROPE_DIMS)
    ang = pos[:, None] * inv_freq[None, :]
    return jnp.cos(ang), jnp.sin(ang)


def apply_partial_rope(x, cos, sin):
    half = ROPE_DIMS // 2
    shape = (1, x.shape[1]) + (1,) * (x.ndim - 3) + (half,)
    c = cos.reshape(shape).astype(x.dtype)
    s = sin.reshape(shape).astype(x.dtype)
    x1 = x[..., :half]
    x2 = x[..., half:ROPE_DIMS]
    return jnp.concatenate([x1 * c - x2 * s, x2 * c + x1 * s, x[..., ROPE_DIMS:]], axis=-1)


def rwkv7_time_mix(pa, mu, w0, w2, a0, a2, g2, k_k, k_a, r_k, lnx_w, lnx_b):
    B, L, _ = pa.shape
    H, N = RWKV_HEADS, HEAD_DIM
    f32 = jnp.float32
    prev = jnp.pad(pa, ((0, 0), (1, 0), (0, 0)))[:, :-1]
    pa = pa + (prev - pa) * mu
    r, k, v, wl, al, gl = jnp.split(pa, _offsets(RWKV_SIZES), axis=-1)
    w = -jax.nn.softplus(-(w0 + jnp.tanh(wl) @ w2).astype(f32)) - 0.5
    decay = jnp.exp(-jnp.exp(w))
    a = jax.nn.sigmoid(a0 + al @ a2)
    g = jax.nn.sigmoid(gl) @ g2
    kk = (k * k_k).reshape(B, L, H, N).astype(f32)
    kk = kk / jnp.maximum(jnp.sqrt(jnp.sum(kk * kk, axis=-1, keepdims=True)), 1e-12)
    k = k * (1.0 + (a - 1.0) * k_a)

    def heads(t):
        return t.reshape(B, L, H, N).astype(f32).transpose(1, 0, 2, 3)

    xs = (heads(r), heads(k), heads(v), kk.transpose(1, 0, 2, 3), heads(a), heads(decay))

    def step(S, inp):
        r_t, k_t, v_t, kk_t, a_t, w_t = inp
        sa = jnp.einsum('bhvk,bhk->bhv', S, -kk_t)
        S = (S * w_t[:, :, None, :] + sa[..., None] * (kk_t * a_t)[:, :, None, :]
             + v_t[..., None] * k_t[:, :, None, :])
        return S, jnp.einsum('bhvk,bhk->bhv', S, r_t)

    _, y = lax.scan(step, jnp.zeros((B, H, N, N), f32), xs)
    y = y.transpose(1, 0, 2, 3)
    mean = jnp.mean(y, axis=-1, keepdims=True)
    var = jnp.mean(jnp.square(y - mean), axis=-1, keepdims=True)
    y = ((y - mean) * lax.rsqrt(var + RWKV_GN_EPS)).reshape(B, L, H * N)
    y = y * lnx_w.astype(f32) + lnx_b.astype(f32)
    rh = r.reshape(B, L, H, N).astype(f32)
    kh = k.reshape(B, L, H, N).astype(f32)
    vh = v.reshape(B, L, H, N).astype(f32)
    bonus = jnp.sum(rh * kh * r_k.reshape(H, N).astype(f32), axis=-1, keepdims=True) * vh
    y = (y + bonus.reshape(B, L, H * N)) * g.astype(f32)
    return y.astype(pa.dtype)


def swa_sink_attention(q, k, v, sinks, cos, sin):
    B, L, _ = q.shape
    Hk, G, Dh, Bk = SWA_KV_HEADS, SWA_GROUP, HEAD_DIM, SWA_BLOCK
    q = apply_partial_rope(q.reshape(B, L, Hk, G, Dh), cos, sin)
    k = apply_partial_rope(k.reshape(B, L, Hk, Dh), cos, sin)
    v = v.reshape(B, L, Hk, Dh)
    Lp = L + PAD_FRONT
    nb = Lp // Bk
    qb = jnp.pad(q, ((0, 0), (PAD_FRONT, 0), (0, 0), (0, 0), (0, 0))).reshape(B, nb, Bk, Hk, G, Dh)

    def band(t):
        tb = jnp.pad(t, ((0, 0), (PAD_FRONT, 0), (0, 0), (0, 0))).reshape(B, nb, Bk, Hk, Dh)
        prev = jnp.pad(tb, ((0, 0), (1, 0), (0, 0), (0, 0), (0, 0)))[:, :-1]
        return jnp.concatenate([prev, tb], axis=2)

    k_band, v_band = band(k), band(v)
    k_meta, v_meta = k[:, :N_META], v[:, :N_META]
    s_meta = jnp.einsum('bnqhgd,bmhd->bnhgqm', qb, k_meta)
    s_band = jnp.einsum('bnqhgd,bnkhd->bnhgqk', qb, k_band)
    s = jnp.concatenate([s_meta, s_band], axis=-1).astype(jnp.float32) * (HEAD_DIM ** -0.5)
    blk = jnp.arange(nb)[:, None]
    qpos = blk * Bk + jnp.arange(Bk)[None, :] - PAD_FRONT
    kpos = (blk - 1) * Bk + jnp.arange(2 * Bk)[None, :] - PAD_FRONT
    rel = qpos[:, :, None] - kpos[:, None, :]
    band_ok = (kpos[:, None, :] >= N_META) & (rel >= 0) & (rel < SWA_WINDOW)
    meta_ok = jnp.arange(N_META)[None, None, :] <= qpos[:, :, None]
    mask = jnp.concatenate([meta_ok, band_ok], axis=-1)
    s = jnp.where(mask[None, :, None, None], s, NEG_INF)
    sink = jnp.broadcast_to(sinks.reshape(1, 1, Hk, G, 1, 1).astype(jnp.float32), s.shape[:-1] + (1,))
    p = jax.nn.softmax(jnp.concatenate([s, sink], axis=-1), axis=-1).astype(v.dtype)
    out = (jnp.einsum('bnhgqm,bmhd->bnqhgd', p[..., :N_META], v_meta)
           + jnp.einsum('bnhgqk,bnkhd->bnqhgd', p[..., N_META:-1], v_band))
    return out.reshape(B, Lp, SWA_WIDTH)[:, PAD_FRONT:]


def hgrn2_mix(q, f_raw, i, g, lower_bound, norm_gain):
    B, L, _ = q.shape
    H, Dk, C = HGRN_HEADS, HGRN_HEAD_DIM, HGRN_CHUNK
    f32 = jnp.float32
    f = lower_bound.astype(f32) + (1.0 - lower_bound.astype(f32)) * jax.nn.sigmoid(f_raw.astype(f32))
    log_f = jnp.log(f)
    k = 1.0 - f
    pad = ((0, 0), (PAD_FRONT, 0), (0, 0))
    qp, kp, ip, lp = [jnp.pad(t.astype(f32), pad) for t in (q, k, i, log_f)]
    Lp = L + PAD_FRONT
    nc = Lp // C

    def chunks(t):
        return t.reshape(B, nc, C, H, Dk).transpose(1, 0, 3, 2, 4)

    tri = jnp.tril(jnp.ones((C, C), dtype=bool))

    def step(S, inp):
        qc, kc, ic, lc = inp
        Gc = jnp.cumsum(lc, axis=2)
        diff = Gc[:, :, :, None, :] - Gc[:, :, None, :, :]
        dec = jnp.exp(jnp.where(tri[:, :, None], diff, NEG_INF))
        attn = jnp.einsum('bhtk,bhtsk,bhsk->bhts', qc, dec, kc)
        o = attn @ ic + jnp.einsum('bhtk,bhkv->bhtv', qc * jnp.exp(Gc), S)
        g_last = Gc[:, :, -1:, :]
        S = (S * jnp.exp(g_last[:, :, 0, :])[..., None]
             + jnp.einsum('bhsk,bhsv->bhkv', kc * jnp.exp(g_last - Gc), ic))
        return S, o

    _, o = lax.scan(step, jnp.zeros((B, H, Dk, Dk), f32), (chunks(qp), chunks(kp), chunks(ip), chunks(lp)))
    o = o.transpose(1, 0, 3, 2, 4).reshape(B, Lp, H, Dk)[:, PAD_FRONT:]
    o = rms_norm(o, norm_gain.reshape(H, Dk)).reshape(B, L, HGRN_WIDTH)
    return (o * jax.nn.silu(g.astype(f32))).astype(q.dtype)


def peer_ffn(h, w_q, sub_keys, u_tab, v_tab):
    B, L, D = h.shape
    T = B * L
    t = h.reshape(T, D)
    q = (t @ w_q).reshape(T, PEER_HEADS, 2, PEER_HALF_DIM)
    s = jnp.einsum('thcd,hcnd->thcn', q, sub_keys).astype(jnp.float32)
    s1, i1 = lax.top_k(s[:, :, 0], PEER_TOPK)
    s2, i2 = lax.top_k(s[:, :, 1], PEER_TOPK)
    cand = (s1[..., :, None] + s2[..., None, :]).reshape(T, PEER_HEADS, PEER_TOPK * PEER_TOPK)
    cand_ids = (i1[..., :, None] * PEER_N_KEYS + i2[..., None, :]).reshape(T, PEER_HEADS, PEER_TOPK * PEER_TOPK)
    best, pos = lax.top_k(cand, PEER_TOPK)
    experts = jnp.take_along_axis(cand_ids, pos, axis=-1).reshape(T, PEER_HEADS * PEER_TOPK)
    gates = jax.nn.softmax(best, axis=-1).reshape(T, PEER_HEADS * PEER_TOPK)
    T_pad = -(-T // PEER_TOKEN_BLOCK) * PEER_TOKEN_BLOCK
    n_blk = T_pad // PEER_TOKEN_BLOCK
    extra = T_pad - T
    tb = jnp.pad(t, ((0, extra), (0, 0))).reshape(n_blk, PEER_TOKEN_BLOCK, D)
    eb = jnp.pad(experts, ((0, extra), (0, 0))).reshape(n_blk, PEER_TOKEN_BLOCK, -1)
    gb = jnp.pad(gates, ((0, extra), (0, 0))).reshape(n_blk, PEER_TOKEN_BLOCK, -1)

    def block(args):
        x_b, e_b, g_b = args
        hid = jax.nn.gelu(jnp.einsum('ted,td->te', u_tab[e_b], x_b).astype(jnp.float32), approximate=False)
        return jnp.einsum('te,ted->td', (g_b * hid).astype(v_tab.dtype), v_tab[e_b])

    out = lax.map(block, (tb, eb, gb))
    return out.reshape(T_pad, D)[:T].reshape(B, L, D).astype(h.dtype)


def setup_inputs(seed: int = 0) -> dict:
    key = jax.random.key(seed)
    ks = jax.random.split(key, 25)
    D = D_MODEL
    nrm = jax.random.normal
    f32 = jnp.float32
    return {
        'x': nrm(ks[0], (BATCH, SEQ, D), f32),
        'meta_tokens': nrm(ks[1], (N_META, D), f32),
        'norm_mix': 1.0 + 0.02 * nrm(ks[2], (DEPTH, D), f32),
        'norm_ffn': 1.0 + 0.02 * nrm(ks[3], (DEPTH, D), f32),
        'norm_final': 1.0 + 0.02 * nrm(ks[4], (D,), f32),
        'w_in': nrm(ks[5], (DEPTH, D, IN_COLS), f32) * D ** -0.5,
        'w_out': nrm(ks[6], (DEPTH, D, D), f32) * (0.5 * D ** -0.5),
        'rwkv_mu': jax.random.uniform(ks[7], (DEPTH, RWKV_COLS), f32),
        'rwkv_w0': -2.0 + 0.5 * nrm(ks[8], (DEPTH, RWKV_WIDTH), f32),
        'rwkv_w2': 0.1 * nrm(ks[9], (DEPTH, RWKV_DECAY_RANK, RWKV_WIDTH), f32),
        'rwkv_a0': 0.1 * nrm(ks[10], (DEPTH, RWKV_WIDTH), f32),
        'rwkv_a2': 0.1 * nrm(ks[11], (DEPTH, RWKV_ICLR_RANK, RWKV_WIDTH), f32),
        'rwkv_g2': nrm(ks[12], (DEPTH, RWKV_GATE_RANK, RWKV_WIDTH), f32) * RWKV_GATE_RANK ** -0.5,
        'rwkv_k_k': 0.85 + 0.02 * nrm(ks[13], (DEPTH, RWKV_WIDTH), f32),
        'rwkv_k_a': 1.0 + 0.02 * nrm(ks[14], (DEPTH, RWKV_WIDTH), f32),
        'rwkv_r_k': 0.1 * nrm(ks[15], (DEPTH, RWKV_WIDTH), f32),
        'rwkv_lnx_w': 1.0 + 0.02 * nrm(ks[16], (DEPTH, RWKV_WIDTH), f32),
        'rwkv_lnx_b': 0.01 * nrm(ks[17], (DEPTH, RWKV_WIDTH), f32),
        'swa_sinks': 0.5 * nrm(ks[18], (DEPTH, SWA_Q_HEADS), f32),
        'hgrn_lb': 0.5 * nrm(ks[19], (DEPTH, HGRN_WIDTH), f32),
        'hgrn_norm': 1.0 + 0.02 * nrm(ks[20], (DEPTH, HGRN_WIDTH), f32),
        'peer_wq': nrm(ks[21], (DEPTH, D, PEER_HEADS * PEER_QUERY_DIM), f32) * D ** -0.5,
        'peer_keys': nrm(ks[22], (DEPTH, PEER_HEADS, 2, PEER_N_KEYS, PEER_HALF_DIM), f32) * PEER_HALF_DIM ** -0.5,
        'peer_u': nrm(ks[23], (DEPTH, PEER_N_EXPERTS, D), f32) * D ** -0.5,
        'peer_v': 0.1 * nrm(ks[24], (DEPTH, PEER_N_EXPERTS, D), f32),
    }


def reference(x, meta_tokens, norm_mix, norm_ffn, norm_final, w_in, w_out,
              rwkv_mu, rwkv_w0, rwkv_w2, rwkv_a0, rwkv_a2, rwkv_g2, rwkv_k_k, rwkv_k_a, rwkv_r_k,
              rwkv_lnx_w, rwkv_lnx_b, swa_sinks, hgrn_lb, hgrn_norm,
              peer_wq, peer_keys, peer_u, peer_v):
    B = x.shape[0]
    meta = jnp.broadcast_to(meta_tokens.astype(x.dtype)[None], (B, N_META, D_MODEL))
    h = jnp.concatenate([meta, x], axis=1)
    L = h.shape[1]
    cos, sin = rope_tables(L)
    lb_all = jnp.cumsum(jax.nn.softmax(hgrn_lb.astype(jnp.float32), axis=0), axis=0)
    lb_all = lb_all - lb_all[0:1]
    for l in range(DEPTH):
        z = rms_norm(h, norm_mix[l])
        proj = z @ w_in[l]
        pa = proj[..., :RWKV_COLS]
        sq, sk, sv, hq, hf, hi, hg = jnp.split(proj[..., RWKV_COLS:], _offsets(SWA_SIZES + HGRN_SIZES), axis=-1)
        y_a = rwkv7_time_mix(pa, rwkv_mu[l], rwkv_w0[l], rwkv_w2[l], rwkv_a0[l], rwkv_a2[l], rwkv_g2[l],
                             rwkv_k_k[l], rwkv_k_a[l], rwkv_r_k[l], rwkv_lnx_w[l], rwkv_lnx_b[l])
        y_b = swa_sink_attention(sq, sk, sv, swa_sinks[l], cos, sin)
        y_c = hgrn2_mix(hq, hf, hi, hg, lb_all[l], hgrn_norm[l])
        mixed = jnp.concatenate([y_a.astype(h.dtype), y_b.astype(h.dtype), y_c.astype(h.dtype)], axis=-1)
        h = h + mixed @ w_out[l]
        h = h + peer_ffn(rms_norm(h, norm_ffn[l]), peer_wq[l], peer_keys[l], peer_u[l], peer_v[l])
    out = rms_norm(h, norm_final)[:, N_META:]
    return out
```

```python
import math
import numpy as np
import ml_dtypes
import concourse.bass as bass
import concourse.mybir as mybir
from concourse.bass_utils import run_bass_kernel_spmd

F32 = mybir.dt.float32
BF16 = mybir.dt.bfloat16
U32 = mybir.dt.uint32
U16 = mybir.dt.uint16
I32 = mybir.dt.int32
ALU = mybir.AluOpType
AF = mybir.ActivationFunctionType
AX = mybir.AxisListType

_DT_SIZE = {F32: 4, BF16: 2, U32: 4, I32: 4, U16: 2, mybir.dt.float32r: 4,
            mybir.dt.uint8: 1, mybir.dt.int16: 2, mybir.dt.float16: 2}


def _box(ap):
    t = ap.tensor
    name = t.name
    esz = _DT_SIZE[ap.dtype]
    dims = list(ap.ap)
    space = str(ap.space) if hasattr(ap, "space") else ""
    if "DRAM" in space.upper() or "HBM" in space.upper() or not hasattr(t, "base_partition"):
        lo = ap.offset
        hi = ap.offset
        for st, n in dims:
            if n > 1:
                if st >= 0:
                    hi += st * (n - 1)
                else:
                    lo += st * (n - 1)
        return (name, 0, 1, lo * esz, (hi + 1) * esz)
    if "PSUM" in space.upper():
        return (name, 0, 128, 0, 1 << 30)
    pstep = dims[0][0]
    pn = dims[0][1]
    if pstep == 0:
        pstep = 1 << 40
    p0 = ap.offset // pstep if pstep < (1 << 40) else 0
    rem = ap.offset - p0 * pstep if pstep < (1 << 40) else ap.offset
    lo = rem
    hi = rem
    for st, n in dims[1:]:
        if n > 1:
            if st >= 0:
                hi += st * (n - 1)
            else:
                lo += st * (n - 1)
    return (name, p0, p0 + pn, lo * esz, (hi + 1) * esz)


class Prog:
    ENGS = ("pe", "act", "pool", "dve", "sp")

    def __init__(self, nc, n_dma_sems=24):
        self.nc = nc
        self.q = {e: [] for e in self.ENGS}
        self.cnt = {e: 0 for e in self.ENGS}
        self.sems = {}
        self._sem_ctx = []
        for e in self.ENGS:
            self.sems[e] = self._mksem("c_" + e)
        self.dma_sems = [self._mksem("d%d" % i) for i in range(n_dma_sems)]
        self.dma_cnt = [0] * n_dma_sems
        self.dma_rr = 0
        self.waited = {e: {} for e in self.ENGS}
        self.recs = {}
        self.n_wait = 0
        self.n_inst = 0
        self._tensors = []

    def _mksem(self, name):
        cm = self.nc.semaphore(name)
        s = cm.__enter__()
        self._sem_ctx.append(cm)
        return s

    def sb(self, name, shape, dt):
        cm = self.nc.sbuf_tensor(name, list(shape), dt)
        t = cm.__enter__()
        self._tensors.append(cm)
        return t

    def ps(self, name, shape, dt):
        cm = self.nc.psum_tensor(name, list(shape), dt)
        t = cm.__enter__()
        self._tensors.append(cm)
        return t

    @staticmethod
    def _ovl(a, b):
        return a[1] < b[2] and b[1] < a[2] and a[3] < b[4] and b[3] < a[4]

    @staticmethod
    def _contains(a, b):
        return a[1] <= b[1] and a[2] >= b[2] and a[3] <= b[3] and a[4] >= b[4]

    def _deps_for(self, reads, writes):
        deps = set()
        for ap in reads:
            bx = _box(ap)
            for r in self.recs.get(bx[0], ()):
                if r[1] == "w" and self._ovl(r[0], bx):
                    deps.add(r[2])
        for ap in writes:
            bx = _box(ap)
            for r in self.recs.get(bx[0], ()):
                if self._ovl(r[0], bx):
                    deps.add(r[2])
        return deps

    def _record(self, reads, writes, dep):
        for ap in writes:
            bx = _box(ap)
            lst = self.recs.setdefault(bx[0], [])
            lst[:] = [r for r in lst if not self._contains(bx, r[0])]
            lst.append([bx, "w", dep])
        for ap in reads:
            bx = _box(ap)
            lst = self.recs.setdefault(bx[0], [])
            done = False
            for r in lst:
                if r[1] == "r" and r[0] == bx and r[2][0] == dep[0]:
                    if dep[1] > r[2][1]:
                        r[2] = dep
                    done = True
                    break
            if not done:
                lst.append([bx, "r", dep])

    def _emit_waits(self, eng, deps):
        w = self.waited[eng]
        best = {}
        for (sk, val) in deps:
            if val > best.get(sk, 0):
                best[sk] = val
        for sk, val in best.items():
            if w.get(sk, 0) >= val:
                continue
            w[sk] = val
            sem = self._sem_of(sk)
            self.q[eng].append(lambda e, sem=sem, val=val: e.wait_ge(sem, val))
            self.n_wait += 1

    def _sem_of(self, sk):
        if isinstance(sk, str):
            return self.sems[sk]
        return self.dma_sems[sk]

    def op(self, eng, fn, reads=(), writes=(), extra_deps=()):
        deps = self._deps_for(reads, writes)
        deps.update(extra_deps)
        if eng == "pe":
            deps = {d for d in deps if d[0] != "pe"}
        self._emit_waits(eng, deps)
        self.cnt[eng] += 1
        dep = (eng, self.cnt[eng])
        sem = self.sems[eng]
        self.q[eng].append(lambda e, fn=fn, sem=sem: fn(e).then_inc(sem, 1))
        self._record(reads, writes, dep)
        self.n_inst += 1
        return dep

    def dma(self, out, in_, queue="sp", **kw):
        deps = self._deps_for([in_], [out])
        k = self.dma_rr
        self.dma_rr = (self.dma_rr + 1) % len(self.dma_sems)
        if self.dma_cnt[k] > 0:
            deps.add((k, 16 * self.dma_cnt[k]))
        self._emit_waits(queue, deps)
        self.dma_cnt[k] += 1
        dep = (k, 16 * self.dma_cnt[k])
        sem = self.dma_sems[k]
        self.q[queue].append(
            lambda e, out=out, in_=in_, sem=sem, kw=kw: e.dma_start(out=out, in_=in_, **kw).then_inc(sem, 16))
        self._record([in_], [out], dep)
        self.n_inst += 1
        return dep

    def finish(self):
        deps = set()
        for k, c in enumerate(self.dma_cnt):
            if c:
                deps.add((k, 16 * c))
        for e in self.ENGS:
            if e != "sp" and self.cnt[e]:
                deps.add((e, self.cnt[e]))
        self._emit_waits("sp", deps)
        nc = self.nc
        q = self.q
        with nc.Block() as block:
            @block.sync
            def _(e):
                for f in q["sp"]:
                    f(e)

            @block.tensor
            def _(e):
                for f in q["pe"]:
                    f(e)

            @block.vector
            def _(e):
                for f in q["dve"]:
                    f(e)

            @block.scalar
            def _(e):
                for f in q["act"]:
                    f(e)

            @block.gpsimd
            def _(e):
                for f in q["pool"]:
                    f(e)
        for cm in reversed(self._tensors):
            cm.__exit__(None, None, None)
        for cm in reversed(self._sem_ctx):
            cm.__exit__(None, None, None)


SWA_LEVEL = 9
SWA_SUB = ''
D = 4096
T = 512
NCHK = 8
C = 64
NCH_W = 47
PADF = 112
C0 = math.exp(-0.5)
LN_THETA = math.log(500000.0)
TWO_PI = 2.0 * math.pi
CW1 = 6.28125
CW2 = TWO_PI - CW1


def build_A(NT, parts=("rwkv", "swa", "hgrn"), simw=False):
    nc = bass.Bass("TRN2", target_bir_lowering=False)
    NTOK = NT * T
    dt = nc.dram_tensor
    hT = dt("hT", [D, NTOK], F32, kind="ExternalInput").ap()
    Wc = dt("Wc", [NCH_W, 128, 32, 128], F32, kind="ExternalInput").ap()
    mixT = dt("mixT", [2048, NTOK], BF16, kind="ExternalOutput").ap()

    P = Prog(nc)
    sb, ps = P.sb, P.ps

    def V(fn, reads, writes):
        return P.op("dve", fn, reads=reads, writes=writes)

    def A(fn, reads, writes):
        return P.op("act", fn, reads=reads, writes=writes)

    def mm(out, lhsT, rhs, start=True, stop=True):
        return P.op("pe", lambda e: e.matmul(out, lhsT=lhsT, rhs=rhs, start=start, stop=stop),
                    reads=[lhsT, rhs], writes=[out])

    def tr(out, in_, ident):
        return P.op("pe", lambda e: e.transpose(out=out, in_=in_, identity=ident), reads=[in_, ident], writes=[out])

    def tt(out, in0, in1, op):
        return V(lambda e: e.tensor_tensor(out=out, in0=in0, in1=in1, op=op), [in0, in1], [out])

    def ts(out, in0, s1, op0, s2=None, op1=None):
        rd = [in0] + [s for s in (s1, s2) if not isinstance(s, (int, float, type(None)))]
        if op1 is None:
            return V(lambda e: e.tensor_scalar(out=out, in0=in0, scalar1=s1, scalar2=None, op0=op0), rd, [out])
        return V(lambda e: e.tensor_scalar(out=out, in0=in0, scalar1=s1, scalar2=s2, op0=op0, op1=op1), rd, [out])

    def stt(out, in0, s, in1, op0, op1):
        rd = [in0, in1] + ([] if isinstance(s, (int, float)) else [s])
        return V(lambda e: e.scalar_tensor_tensor(out=out, in0=in0, scalar=s, in1=in1, op0=op0, op1=op1), rd, [out])

    def act(out, in_, func, bias=None, scale=None):
        rd = [in_] + ([] if isinstance(bias, (int, float, type(None))) else [bias])
        kw = {}
        if bias is not None:
            kw["bias"] = bias
        if scale is not None:
            kw["scale"] = scale
        return A(lambda e: e.activation(out=out, in_=in_, func=func, **kw), rd, [out])

    def vcopy(out, in_):
        return V(lambda e: e.tensor_copy(out=out, in_=in_), [in_], [out])

    def acopy(out, in_):
        return A(lambda e: e.copy(out=out, in_=in_), [in_], [out])

    def v3(ap, n=64):
        return ap.rearrange("p (j t) -> p j t", t=n)

    stage = sb("stage", [128, 768], F32)

    def ld_const(name, shape, dtype=F32):
        d_ = dt(name, list(shape), F32, kind="ExternalInput").ap()
        if dtype == F32:
            t_ = sb(name + "_s", list(shape), F32)
            P.dma(t_[:], d_)
            return t_
        n = int(np.prod(shape[1:]))
        st = stage[0:shape[0], 0:n]
        if len(shape) == 3:
            st = st.rearrange("p (a b) -> p a b", a=shape[1])
        P.dma(st, d_)
        tb_ = sb(name + "_b", list(shape), dtype)
        vcopy(tb_[:], st)
        return tb_
    pv = ld_const("pv", [128, 80])
    gn = ld_const("gain", [128, 32])
    lwb = ld_const("lw", [128, 768], BF16)
    g2b = ld_const("g2", [128, 768], BF16)
    identb = ld_const("c_ident", [128, 128], BF16)
    blkb = ld_const("c_blk", [128, 128], BF16)
    pitb = ld_const("c_pit", [128, 128], BF16)
    rmask_t = ld_const("c_rmask", [128, 512])
    rmask = rmask_t[:]
    tri = ld_const("c_tri", [64, 4, 64], BF16)
    su512, sl512, ui512, id512 = (tri[:, i, :].unsqueeze(1).to_broadcast([64, NCHK, 64]) for i in range(4))
    amask = ld_const("c_amask", [128, 2, 128], BF16)
    mprev, mcur = (amask[:, i, :].unsqueeze(1).to_broadcast([128, 4, 128]) for i in range(2))
    mmeta0_t = ld_const("c_mmeta0", [16, 128], BF16)
    mmeta0 = mmeta0_t[:].unsqueeze(1).to_broadcast([16, 4, 128])
    cpos = ld_const("c_pos", [128, 512])
    ceps = ld_const("c_eps", [128, 2])
    snk = ld_const("sinks", [128, 12])
    onesb = sb("onesb", [128, 128], BF16)
    V(lambda e: e.memset(onesb[:], 1.0), [], [onesb[:]])
    MU0 = 0
    W0, A0, KK, KA, RK, LW_, LB_ = 20, 26, 32, 38, 44, 50, 56
    HLB0, HLBL, HNG, LBF, JIDX, RMK, SSG = 62, 66, 70, 74, 75, 76, 77
    omka = sb("omka", [128, 6], F32)
    ts(omka[:], pv[:, KA:KA + 6], -1.0, ALU.mult, 1.0, ALU.add)
    hlb = sb("hlb", [128, 4], F32)
    hom = sb("hom", [128, 4], F32)
    tt(hlb[:], pv[:, HLBL:HLBL + 4], pv[:, HLB0:HLB0 + 4], ALU.subtract)
    act(hlb[:], hlb[:], AF.Sigmoid)
    ts(hlb[:], hlb[:], pv[:, LBF:LBF + 1], ALU.mult)
    ts(hom[:], hlb[:], -1.0, ALU.mult, 1.0, ALU.add)
    invf = sb("invf", [128, 1], F32)
    act(invf[:], pv[:, JIDX:JIDX + 1], AF.Exp, scale=-LN_THETA / 8.0)
    omr = sb("omr", [128, 1], F32)
    ts(omr[:], pv[:, RMK:RMK + 1], -1.0, ALU.mult, 1.0, ALU.add)
    expsink = sb("expsink", [128, 12], F32)
    act(expsink[:], snk[:], AF.Exp)

    hg = sb("hg", [128, 32, T], BF16)
    hst = [sb("hst%d" % i, [128, T], F32) for i in range(2)]
    sqb = [sb("sqb%d" % i, [128, T], BF16) for i in range(2)]
    rstd = sb("rstd", [128, T], F32)
    NWB = 2 if simw else 3
    wb = [sb("wb%d" % i, [128, 32, 128], BF16) for i in range(NWB)]
    wst_ = sb("wst_", [128, 16, 128], F32) if simw else None
    pj = [ps("pj%d" % i, [128, T], F32) for i in range(2)]
    pA = ps("pA", [128, T], F32)
    pB = ps("pB", [128, T], F32)
    pC = ps("pC", [128, T], F32)
    pD = ps("pD", [128, T], F32)
    pE = ps("pE", [128, T], F32)
    pF = ps("pF", [128, T], F32)

    FP = [sb("FP%d" % i, [128, T + 1], F32) for i in range(21)]
    BP = [sb("BP%d" % i, [128, T], BF16) for i in range(17)]
    TMP = [sb("TMP%d" % i, [64, NCHK, 128], BF16) for i in range(3)]
    YS = [sb("YS%d" % i, [64, 2, T], F32) for i in range(2)]
    YNt = sb("YNt", [64, NCHK, 128], BF16)
    MAT = [sb("MAT%d" % i, [64, T], BF16) for i in range(14)]

    class _V3:
        def __init__(self, t):
            self.t = t
        def __getitem__(self, k):
            return self.t[:].rearrange("p h (j v) -> p (h j) v", v=128)[k]

    class _V:
        def __init__(self, t, n=T):
            self.t = t; self.n = n
        def __getitem__(self, k):
            if isinstance(k, tuple):
                return self.t[:, 0:self.n][k]
            return self.t[:, 0:self.n][k]

    wlist = []
    for ti in range(NT):
        for c in range(NCH_W):
            if c < 20 and "rwkv" not in parts:
                continue
            if 20 <= c < 31 and "swa" not in parts:
                continue
            if c >= 31 and "hgrn" not in parts:
                continue
            wlist.append((ti, c))
    wstate = {"issued": 0, "used": 0}

    def w_prefetch(upto):
        while wstate["issued"] < min(upto, len(wlist)):
            i = wstate["issued"]
            _, c = wlist[i]
            if simw:
                for hf_ in range(2):
                    P.dma(wst_[:], Wc[c][:, 16 * hf_:16 * hf_ + 16, :], queue="sp")
                    vcopy(wb[i % NWB][:, 16 * hf_:16 * hf_ + 16, :], wst_[:])
            else:
                P.dma(wb[i % NWB][:], Wc[c], queue="pool", max_dma_last_dim=8192)
            wstate["issued"] += 1

    def proj(ti, c):
        i = wstate["used"]
        assert wlist[i] == (ti, c), (wlist[i], ti, c)
        w_prefetch(i + NWB)
        p = pj[i % 2]
        w = wb[i % NWB]
        for k in range(32):
            mm(p[:], w[:, k, :], hg[:, k, :], start=(k == 0), stop=(k == 31))
        wstate["used"] += 1
        return p

    if "rwkv" in parts:
        carry = sb("carry", [128, 20], F32)
        V(lambda e: e.memset(carry[:], 0.0), [], [carry[:]])
        X = FP[0:3]
        xs_r, xs_k, xs_v, dtmp, sig, Gp, Gx, E1, E3, E4, aa, kk, nrm, kap, tmpf, kmod, bonus, gsb = (_V(t) for t in FP[3:21])
        E2 = Gx
        beta = kk
        twal, sgb, kk2, rkb, KP, NB, KT, KH, NBH, VB = BP[0:10]
        RTz = BP[10:12]
        KPz = BP[12:14]
        NBz = BP[14:16]
        for z_ in (RTz, KPz, NBz):
            for t_ in z_:
                V(lambda e, t_=t_: e.memset(t_[:], 0.0), [], [t_[:]])
        KH_TM, NBH_TM, V_TM = TMP
        Nm = [[MAT[0], MAT[1]], [MAT[0], MAT[1]]]
        Mm = [[MAT[2], MAT[3]], [MAT[2], MAT[3]]]
        Um = [[MAT[4], MAT[5]], [MAT[6], MAT[7]]]
        MKT = MAT[8:10]
        ARK = MAT[10:12]
        NARB = MAT[12:14]
        S32 = [sb("S32_%d" % hp, [128, 64], F32) for hp in range(6)]
        Sbf = [sb("Sbf_%d" % hp, [128, 64], BF16) for hp in range(6)]
        for hp in range(6):
            V(lambda e, hp=hp: e.memset(S32[hp][:], 0.0), [], [S32[hp][:]])
            V(lambda e, hp=hp: e.memset(Sbf[hp][:], 0.0), [], [Sbf[hp][:]])
        Xsb = [sb("Xsb%d" % h, [64, 64], BF16) for h in range(2)]
        Psb = [sb("Psb%d" % h, [64, 64], BF16) for h in range(2)]
        Ysb, Ysq = YS
        gst = sb("gst", [64, 64], F32)
        YN = YNt
        y1 = dtmp
        yo = BP[16]

    if "swa" in parts:
        KT2 = [sb("KT2_%d" % s, [128, 128 + T], BF16) for s in range(3)]
        KM = [sb("KM_%d" % s, [128, 16], BF16) for s in range(3)]
        Vaug = sb("Vaug", [128, 5, 4, 65], BF16)
        VMaug = sb("VMaug", [16, 4, 65], BF16)
        V(lambda e: e.memset(Vaug[:], 1.0), [], [Vaug[:]])
        V(lambda e: e.memset(VMaug[:], 1.0), [], [VMaug[:]])
        for s in range(3):
            V(lambda e, s=s: e.memset(KT2[s][:], 0.0), [], [KT2[s][:]])
        QTz = sb("QTz", [128, 12, T], BF16)
        V(lambda e: e.memset(QTz[:], 0.0), [], [QTz[:]])
        ctab, stab, ang, angk, posr, xf, r1, r2 = (_V(t) for t in FP[0:8])
        angi = _V(FP[8].bitcast(I32) if hasattr(FP[8], "bitcast") else FP[8])
        xfb = BP[0]
        Eg = BP[1:4]
        Osb = sb("Osb", [128, 4, 65], F32)
        den = sb("den", [128, 4], F32)
        On = sb("On", [128, 4, 64], BF16)
        so = sb("so", [128, 6, T], BF16)

    if "hgrn" in parts:
        hq, hf, hi_, hgx, hsg, hlf, hkf, hG, hD, hE, hEG, hsil = (_V(t) for t in FP[0:12])
        hqg, hkg, hqG, hkd, hib, ho = BP[0:6]
        KD_TM, I_TM = TMP[0:2]
        AT = MAT[0]
        HS32 = [sb("HS32_%d" % h, [128, 128], F32) for h in range(4)]
        HSbf = [sb("HSbf_%d" % h, [128, 128], BF16) for h in range(4)]
        for h in range(4):
            V(lambda e, h=h: e.memset(HS32[h][:], 0.0), [], [HS32[h][:]])
            V(lambda e, h=h: e.memset(HSbf[h][:], 0.0), [], [HSbf[h][:]])
        Osb_h = _V3(YS[0])
        Osq_h = _V3(YS[1])
        hst_ = sb("hst_", [64, 32], F32)
        ON = YNt

    for ti in range(NT):
        t0 = ti * T
        for k in range(32):
            h_ = hst[k % 2]
            P.dma(h_[:], hT[k * 128:(k + 1) * 128, t0:t0 + T])
            act(sqb[k % 2][:], h_[:], AF.Square)
            ts(hg[:, k, :], h_[:], gn[:, k:k + 1], ALU.mult)
            mm(pA[:], onesb[:], sqb[k % 2][:], start=(k == 0), stop=(k == 31))
        act(rstd[:], pA[:], AF.Sqrt, bias=ceps[:, 0:1], scale=1.0 / D)
        V(lambda e: e.reciprocal(out=rstd[:], in_=rstd[:]), [rstd[:]], [rstd[:]])

        def evac_shift(c, p, Xb, dst):
            vcopy(Xb[:, 0:1], carry[:, c:c + 1])
            tt(Xb[:, 1:T + 1], p[:], rstd[:], ALU.mult)
            vcopy(carry[:, c:c + 1], Xb[:, T:T + 1])
            tt(dtmp[:], Xb[:, 0:T], Xb[:, 1:T + 1], ALU.subtract)
            stt(dst, dtmp[:], pv[:, MU0 + c:MU0 + c + 1], Xb[:, 1:T + 1], ALU.mult, ALU.add)

        if "rwkv" in parts:
            p = proj(ti, 0)
            evac_shift(0, p, X[0], xs_r[:])
            act(twal[0:64, :], xs_r[0:64, :], AF.Tanh)
            acopy(twal[64:128, :], xs_r[64:128, :])
            p = proj(ti, 1)
            evac_shift(1, p, X[1], xs_k[:])
            act(sgb[:], xs_k[:], AF.Sigmoid)
            for hp in range(6):
                cs = slice(hp * 128, (hp + 1) * 128)
                pcol = lambda base: pv[:, base + hp:base + hp + 1]
                p = proj(ti, 2 + 3 * hp)
                evac_shift(2 + 3 * hp, p, X[0], xs_r[:])
                p = proj(ti, 3 + 3 * hp)
                evac_shift(3 + 3 * hp, p, X[1], xs_k[:])
                p = proj(ti, 4 + 3 * hp)
                evac_shift(4 + 3 * hp, p, X[2], xs_v[:])
                mm(pB[:], lwb[0:64, cs], twal[0:64, :])
                act(sig[:], pB[:], AF.Sigmoid, bias=pcol(W0))
                V(lambda e: e.tensor_tensor_scan(out=Gp[:], data0=rmask, data1=sig[:], initial=0.0,
                                                 op0=ALU.mult, op1=ALU.add), [rmask, sig[:]], [Gp[:]])
                tt(Gx[:], Gp[:], sig[:], ALU.subtract)
                act(E1[:], Gp[:], AF.Exp, scale=-C0)
                act(E3[:], Gp[:], AF.Exp, scale=C0)
                act(E2[:], Gx[:], AF.Exp, scale=-C0)
                tt(v3(E4[:]), v3(E3[:]), v3(E1[:])[:, :, 63:64].to_broadcast([128, NCHK, 64]), ALU.mult)
                mm(pC[:], lwb[64:128, cs], twal[64:128, :])
                act(aa[:], pC[:], AF.Sigmoid, bias=pcol(A0))
                mm(pD[:], g2b[:, cs], sgb[:])
                acopy(gsb[:], pD[:])
                ts(kk[:], xs_k[:], pcol(KK), ALU.mult)
                tt(kk2[:], kk[:], kk[:], ALU.mult)
                mm(pB[:], blkb[:], kk2[:])
                act(nrm[:], pB[:], AF.Sqrt)
                ts(nrm[:], nrm[:], 1e-12, ALU.max)
                V(lambda e: e.reciprocal(out=nrm[:], in_=nrm[:]), [nrm[:]], [nrm[:]])
                tt(kap[:], kk[:], nrm[:], ALU.mult)
                ts(tmpf[:], aa[:], pcol(KA), ALU.mult, omka[:, hp:hp + 1], ALU.add)
                tt(kmod[:], xs_k[:], tmpf[:], ALU.mult)
                tt(beta[:], aa[:], kap[:], ALU.mult)
                stt(rkb[:], xs_r[:], pcol(RK), kmod[:], ALU.mult, ALU.mult)
                mm(pC[:], blkb[:], rkb[:])
                tt(bonus[:], pC[:], xs_v[:], ALU.mult)
                tt(KP[:], kap[:], E2[:], ALU.mult)
                stt(NB[:], beta[:], -1.0, E3[:], ALU.mult, ALU.mult)
                for hd_ in range(2):
                    hsl_ = slice(hd_ * 64, hd_ * 64 + 64)
                    tt(RTz[hd_][hsl_, :], xs_r[hsl_, :], E1[hsl_, :], ALU.mult)
                    vcopy(KPz[hd_][hsl_, :], KP[hsl_, :])
                    vcopy(NBz[hd_][hsl_, :], NB[hsl_, :])
                tt(KT[:], kmod[:], E3[:], ALU.mult)
                tt(KH[:], kmod[:], E4[:], ALU.mult)
                stt(NBH[:], beta[:], -1.0, E4[:], ALU.mult, ALU.mult)
                vcopy(VB[:], xs_v[:])
                for src, dst, pp in ((KH, KH_TM, pB), (NBH, NBH_TM, pC), (VB, V_TM, pD)):
                    pb16 = pp[0:64, :].bitcast(BF16)
                    for j in range(NCHK):
                        tr(pb16[:, j * 128:(j + 1) * 128], src[:, j * 64:(j + 1) * 64], identb[:])
                    acopy(dst[:].rearrange("p j k -> p (j k)"), pb16[:, 0:NCHK * 128])
                for hd in range(2):
                    pb_ = slice(hd * 64, hd * 64 + 64)

                    def blkmm(pp, L, R):
                        for j in range(NCHK):
                            js = slice(j * 64, (j + 1) * 64)
                            mm(pp[0:64, js], L[:, js], R[:, js])
                    n0, m0, u0 = Nm[hd][0], Mm[hd][0], Um[hd][0]
                    blkmm(pB, NB[:], KPz[hd][:])
                    tt(v3(n0[:]), v3(pB[0:64, :]), su512, ALU.mult)
                    blkmm(pC, KP[:], NBz[hd][:])
                    tt(v3(m0[:]), v3(pC[0:64, :]), sl512, ALU.mult)
                    tt(v3(u0[:]), v3(n0[:]), id512, ALU.add)
                    blkmm(pD, KT[:], KPz[hd][:])
                    tt(v3(MKT[hd][:]), v3(pD[0:64, :]), su512, ALU.mult)
                    blkmm(pB, KT[:], RTz[hd][:])
                    tt(v3(ARK[hd][:]), v3(pB[0:64, :]), ui512, ALU.mult)
                    blkmm(pC, NB[:], RTz[hd][:])
                    tt(v3(NARB[hd][:]), v3(pC[0:64, :]), ui512, ALU.mult)
                    cur = 0
                    for lev in range(1, 6):
                        nprev, mprev_, uprev = Nm[hd][cur], Mm[hd][cur], Um[hd][cur]
                        nnew, mnew, unew = Nm[hd][1 - cur], Mm[hd][1 - cur], Um[hd][1 - cur]
                        if lev < 5:
                            blkmm(pB, mprev_[:], nprev[:])
                        blkmm(pC, nprev[:], mprev_[:])
                        if lev < 5:
                            acopy(nnew[:], pB[0:64, :])
                        vcopy(mnew[:], pC[0:64, :])
                        for j in range(NCHK):
                            js = slice(j * 64, (j + 1) * 64)
                            mm(pD[0:64, js], mnew[:, js], uprev[:, js], start=True, stop=False)
                            mm(pD[0:64, js], identb[0:64, 0:64], uprev[:, js], start=False, stop=True)
                        vcopy(unew[:], pD[0:64, :])
                        cur = 1 - cur
                    assert cur == 1
                Ufin = [Um[0][1], Um[1][1]]
                sq_ps = [pE, pF]
                for j in range(NCHK):
                    js = slice(j * 64, (j + 1) * 64)
                    for hd in range(2):
                        pb_ = slice(hd * 64, hd * 64 + 64)
                        hs = slice(hd * 64, hd * 64 + 64)
                        q = sq_ps[hd]
                        mm(q[0:64, 0:64], KPz[hd][:, js], Sbf[hp][:, :], start=True, stop=False)
                        mm(q[0:64, 0:64], MKT[hd][:, js], V_TM[:, j, hs], start=False, stop=True)
                    for hd in range(2):
                        acopy(Xsb[hd][:], sq_ps[hd][0:64, 0:64])
                    for hd in range(2):
                        mm(sq_ps[hd][0:64, 64:128], Ufin[hd][:, js], Xsb[hd][:])
                    for hd in range(2):
                        vcopy(Psb[hd][:], sq_ps[hd][0:64, 64:128])
                    for hd in range(2):
                        pb_ = slice(hd * 64, hd * 64 + 64)
                        hs = slice(hd * 64, hd * 64 + 64)
                        yp = (pB, pC)[hd]
                        mm(yp[0:64, js], RTz[hd][:, js], Sbf[hp][:, :], start=True, stop=False)
                        mm(yp[0:64, js], ARK[hd][:, js], V_TM[:, j, hs], start=False, stop=False)
                        mm(yp[0:64, js], NARB[hd][:, js], Psb[hd][:], start=False, stop=True)
                        q = sq_ps[hd]
                        mm(q[pb_, 128:192], KH_TM[:, j, hs], V_TM[:, j, hs], start=True, stop=False)
                        mm(q[pb_, 128:192], NBH_TM[:, j, hs], Psb[hd][:], start=False, stop=True)
                    for hd in range(2):
                        pb_ = slice(hd * 64, hd * 64 + 64)
                        stt(S32[hp][pb_, :], S32[hp][pb_, :], E1[pb_, j * 64 + 63:j * 64 + 64], sq_ps[hd][pb_, 128:192],
                            ALU.mult, ALU.add)
                        acopy(Sbf[hp][pb_, :], S32[hp][pb_, :])
                for hd in range(2):
                    yp = (pB, pC)[hd]
                    acopy(Ysb[:, hd, :], yp[0:64, :])
                    act(Ysq[:, hd, :], yp[0:64, :], AF.Square)
                ysb4 = Ysb[:].rearrange("p h (j v) -> p (h j) v", v=64)
                ysq4 = Ysq[:].rearrange("p h (j v) -> p (h j) v", v=64)
                V(lambda e: e.tensor_reduce(out=gst[:, 0:16], in_=ysb4, axis=AX.X, op=ALU.add), [Ysb[:]], [gst[:, 0:16]])
                V(lambda e: e.tensor_reduce(out=gst[:, 16:32], in_=ysq4, axis=AX.X, op=ALU.add), [Ysq[:]], [gst[:, 16:32]])
                ts(gst[:, 0:16], gst[:, 0:16], 1.0 / 64, ALU.mult)
                tt(gst[:, 32:48], gst[:, 0:16], gst[:, 0:16], ALU.mult)
                stt(gst[:, 16:32], gst[:, 16:32], 1.0 / 64, gst[:, 32:48], ALU.mult, ALU.subtract)
                act(gst[:, 16:32], gst[:, 16:32], AF.Sqrt, bias=ceps[0:64, 1:2])
                V(lambda e: e.reciprocal(out=gst[:, 16:32], in_=gst[:, 16:32]), [gst[:, 16:32]], [gst[:, 16:32]])
                tt(ysb4, ysb4, gst[:, 0:16].unsqueeze(2).to_broadcast([64, 16, 64]), ALU.subtract)
                for hd in range(2):
                    tt(YN[:, :, hd * 64:(hd + 1) * 64], Ysb[:, hd, :].rearrange("p (j v) -> p j v", v=64),
                       gst[:, 16 + hd * 8:24 + hd * 8].unsqueeze(2).to_broadcast([64, NCHK, 64]), ALU.mult)
                pd16 = pD[:, :].bitcast(BF16)
                for j in range(NCHK):
                    tr(pd16[:, j * 64:(j + 1) * 64], YN[:, j, :], identb[0:64, 0:64])
                ts(y1[:], pd16[:, 0:T], pcol(LW_), ALU.mult, pcol(LB_), ALU.add)
                tt(y1[:], y1[:], bonus[:], ALU.add)
                tt(yo[:], y1[:], gsb[:], ALU.mult)
                P.dma(mixT[hp * 128:(hp + 1) * 128, t0:t0 + T], yo[:], queue="sp")


        if "swa" in parts:
            ts(posr[:], cpos[:], float(t0 - PADF), ALU.add)
            for tab, off in ((stab, 0.0), (ctab, math.pi / 2)):
                ts(ang[:], posr[:], invf[:, 0:1], ALU.mult, off, ALU.add)
                ts(angk[:], ang[:], 1.0 / TWO_PI, ALU.mult)
                vcopy(angi[:], angk[:])
                vcopy(angk[:], angi[:])
                stt(r1[:], angk[:], -CW1, ang[:], ALU.mult, ALU.add)
                stt(r1[:], angk[:], -CW2, r1[:], ALU.mult, ALU.add)
                ts(r2[:], r1[:], math.pi, ALU.is_gt, -TWO_PI, ALU.mult)
                tt(r1[:], r1[:], r2[:], ALU.add)
                ts(r2[:], r1[:], -math.pi, ALU.is_lt, TWO_PI, ALU.mult)
                tt(r1[:], r1[:], r2[:], ALU.add)
                act(tab[:], r1[:], AF.Sin)
            ts(ctab[:], ctab[:], pv[:, RMK:RMK + 1], ALU.mult, omr[:, 0:1], ALU.add)
            ts(stab[:], stab[:], pv[:, SSG:SSG + 1], ALU.mult)

            if SWA_LEVEL <= 1:
                vcopy(so[:, 0, :], ctab[:]); vcopy(so[:, 1, :], stab[:])
                P.dma(mixT[768:1536, t0:t0 + T].rearrange("(c p) t -> p c t", p=128), so[:], queue="sp")
                continue
            def rope_evac(p, dst, qz=None):
                tt(xf[:], p[:], rstd[:], ALU.mult)
                vcopy(xfb[:], xf[:])
                mm(pB[:], pitb[:], xfb[:])
                tt(r1[:], xf[:], ctab[:], ALU.mult)
                tt(r2[:], pB[:], stab[:], ALU.mult)
                if qz is None:
                    tt(dst, r1[:], r2[:], ALU.add)
                else:
                    tt(QTz[0:64, 2 * qz, :], r1[0:64, :], r2[0:64, :], ALU.add)
                    tt(QTz[64:128, 2 * qz + 1, :], r1[64:128, :], r2[64:128, :], ALU.add)

            for s_ in range(3):
                if ti > 0:
                    vcopy(KT2[s_][:, 0:128], KT2[s_][:, T:T + 128])
                p = proj(ti, 20 + s_)
                rope_evac(p, KT2[s_][:, 128:128 + T])
                if ti == 0:
                    vcopy(KM[s_][:], KT2[s_][:, 128 + PADF:128 + 128])
            if ti > 0:
                vcopy(Vaug[:, 0, :, 0:64], Vaug[:, 4, :, 0:64])
            pc16 = pC[:, :].bitcast(BF16)
            for vi in range(2):
                p = proj(ti, 23 + vi)
                tt(xf[:], p[:], rstd[:], ALU.mult)
                vcopy(xfb[:], xf[:])
                for b_ in range(4):
                    tr(pc16[:, b_ * 128:(b_ + 1) * 128], xfb[:, b_ * 128:(b_ + 1) * 128], identb[:])
                src4 = pc16[:, 0:512].rearrange("p (b s d) -> p b s d", b=4, s=2)
                if vi == 0:
                    vcopy(Vaug[:, 1:5, 0:2, 0:64], src4)
                else:
                    vcopy(Vaug[:, 1:5, 2:3, 0:64], src4[:, :, 0:1, :])
                if ti == 0:
                    pd16_ = pD[:, :].bitcast(BF16)
                    tr(pd16_[0:16, 0:128], xfb[:, PADF:128], identb[:])
                    srcm = pd16_[0:16, 0:128].rearrange("p (s d) -> p s d", s=2)
                    if vi == 0:
                        vcopy(VMaug[0:16, 0:2, 0:64], srcm)
                    else:
                        vcopy(VMaug[0:16, 2:3, 0:64], srcm[:, 0:1, :])
            for qc in range(6):
                p = proj(ti, 25 + qc)
                rope_evac(p, None, qz=qc)
            if SWA_LEVEL <= 2:
                vcopy(so[:], QTz[:, 0:6, :])
                P.dma(mixT[768:1536, t0:t0 + T].rearrange("(c p) t -> p c t", p=128), so[:], queue="sp")
                continue
            for b_ in range(4):
                gb = ti * 4 + b_
                bs = slice(b_ * 128, (b_ + 1) * 128)
                for u in range(3):
                    q4 = QTz[:, 4 * u:4 * u + 4, bs]
                    mm(pB[0:16, :], KM[u][:, 0:16], q4)
                    if gb >= 2:
                        mm(pC[:, :], KT2[u][:, b_ * 128:b_ * 128 + 128], q4)
                    if gb >= 1:
                        mm(pD[:, :], KT2[u][:, 128 + b_ * 128:256 + b_ * 128], q4)
                    if SWA_SUB == 'a':
                        acopy(so[0:16, 2 * u, bs], pB[0:16, 0:128])
                        if gb >= 1:
                            acopy(so[:, 2 * u + 1, bs], pD[:, 0:128])
                        continue
                    act(Eg[0][0:16, :], pB[0:16, :], AF.Exp, scale=0.125)
                    if gb == 0 and SWA_SUB != 'b':
                        tt(v3(Eg[0][0:16, :], 128), v3(Eg[0][0:16, :], 128), mmeta0, ALU.mult)
                    if gb >= 2:
                        act(Eg[1][:], pC[:], AF.Exp, scale=0.125)
                        if SWA_SUB != 'b':
                            tt(v3(Eg[1][:], 128), v3(Eg[1][:], 128), mprev, ALU.mult)
                    if gb >= 1:
                        act(Eg[2][:], pD[:], AF.Exp, scale=0.125)
                        if SWA_SUB != 'b':
                            tt(v3(Eg[2][:], 128), v3(Eg[2][:], 128), mcur, ALU.mult)
                    if SWA_LEVEL <= 3:
                        acopy(so[0:16, 2 * u, bs], Eg[0][0:16, 0:128])
                        continue
                    for hq_ in range(4):
                        hsl = slice(hq_ * 128, (hq_ + 1) * 128)
                        o_ap = pE[:, hq_ * 65:(hq_ + 1) * 65]
                        mm(o_ap, Eg[0][0:16, hsl], VMaug[0:16, u, :], start=True, stop=(gb == 0))
                        if gb >= 2:
                            mm(o_ap, Eg[1][:, hsl], Vaug[:, b_, u, :], start=False, stop=False)
                        if gb >= 1:
                            mm(o_ap, Eg[2][:, hsl], Vaug[:, 1 + b_, u, :], start=False, stop=True)
                    o4 = pE[:, 0:260].rearrange("p (h d) -> p h d", d=65)
                    tt(den[:].unsqueeze(2), o4[:, :, 64:65], expsink[:, 4 * u:4 * u + 4].unsqueeze(2), ALU.add)
                    V(lambda e: e.reciprocal(out=den[:], in_=den[:]), [den[:]], [den[:]])
                    tt(On[:], o4[:, :, 0:64], den[:].unsqueeze(2).to_broadcast([128, 4, 64]), ALU.mult)
                    if SWA_LEVEL <= 4:
                        acopy(so[:, 2 * u, bs], On[:, 0:2, :].rearrange('p h d -> p (h d)'))
                        continue
                    pf16 = pF[:, :].bitcast(BF16)
                    for pi_ in range(2):
                        tr(pf16[:, pi_ * 128:(pi_ + 1) * 128], On[:, 2 * pi_:2 * pi_ + 2, :].rearrange("p h d -> p (h d)"), identb[:])
                    acopy(so[:, 2 * u:2 * u + 2, bs], pf16[:, 0:256].rearrange("p (c q) -> p c q", c=2))
            P.dma(mixT[768:1536, t0:t0 + T].rearrange("(c p) t -> p c t", p=128), so[:], queue="sp")

        if "hgrn" in parts:
            for hd in range(4):
                for ci, dst in enumerate((hq, hf, hi_, hgx)):
                    p = proj(ti, 31 + 4 * hd + ci)
                    tt(dst[:], p[:], rstd[:], ALU.mult)
                act(hsg[:], hf[:], AF.Sigmoid)
                ts(hsg[:], hsg[:], hom[:, hd:hd + 1], ALU.mult, hlb[:, hd:hd + 1], ALU.add)
                act(hlf[:], hsg[:], AF.Ln)
                ts(hkf[:], hsg[:], -1.0, ALU.mult, 1.0, ALU.add)
                V(lambda e: e.tensor_tensor_scan(out=hG[:], data0=rmask, data1=hlf[:], initial=0.0,
                                                 op0=ALU.mult, op1=ALU.add), [rmask, hlf[:]], [hG[:]])
                tt(v3(hD[:]), v3(hG[:]), v3(hG[:])[:, :, 31:32].to_broadcast([128, NCHK, 64]), ALU.subtract)
                act(hE[:], hD[:], AF.Exp)
                tt(hqg[:], hq[:], hE[:], ALU.mult)
                act(hE[:], hD[:], AF.Exp, scale=-1.0)
                tt(hkg[:], hkf[:], hE[:], ALU.mult)
                act(hEG[:], hG[:], AF.Exp)
                tt(hqG[:], hq[:], hEG[:], ALU.mult)
                tt(v3(hD[:]), v3(hG[:]), v3(hG[:])[:, :, 63:64].to_broadcast([128, NCHK, 64]), ALU.subtract)
                act(hE[:], hD[:], AF.Exp, scale=-1.0)
                tt(hkd[:], hkf[:], hE[:], ALU.mult)
                vcopy(hib[:], hi_[:])
                for src, dst, pp in ((hkd, KD_TM, pB), (hib, I_TM, pC)):
                    pb16 = pp[0:64, :].bitcast(BF16)
                    for j in range(NCHK):
                        tr(pb16[:, j * 128:(j + 1) * 128], src[:, j * 64:(j + 1) * 64], identb[:])
                    acopy(dst[:].rearrange("p j k -> p (j k)"), pb16[:, 0:NCHK * 128])
                for j in range(NCHK):
                    js = slice(j * 64, (j + 1) * 64)
                    mm(pD[0:64, js], hkg[:, js], hqg[:, js])
                tt(v3(AT[:]), v3(pD[0:64, :]), ui512, ALU.mult)
                for j in range(NCHK):
                    js = slice(j * 64, (j + 1) * 64)
                    po = (pE, pF)[j // 4]
                    osl = slice((j % 4) * 128, (j % 4) * 128 + 128)
                    mm(po[0:64, osl], AT[:, js], I_TM[:, j, :], start=True, stop=False)
                    mm(po[0:64, osl], hqG[:, js], HSbf[hd][:], start=False, stop=True)
                    mm(pB[:, 0:128], KD_TM[:, j, :], I_TM[:, j, :])
                    stt(HS32[hd][:], HS32[hd][:], hEG[:, j * 64 + 63:j * 64 + 64], pB[:, 0:128], ALU.mult, ALU.add)
                    acopy(HSbf[hd][:], HS32[hd][:])
                for half in range(2):
                    po = (pE, pF)[half]
                    acopy(Osb_h[:, 4 * half:4 * half + 4, :].rearrange("p j v -> p (j v)"), po[0:64, :])
                    act(Osq_h[:, 4 * half:4 * half + 4, :].rearrange("p j v -> p (j v)"), po[0:64, :], AF.Square)
                V(lambda e: e.tensor_reduce(out=hst_[:, 0:8], in_=Osq_h[:], axis=AX.X, op=ALU.add), [Osq_h[:]], [hst_[:, 0:8]])
                act(hst_[:, 0:8], hst_[:, 0:8], AF.Sqrt, bias=ceps[0:64, 0:1], scale=1.0 / 128)
                V(lambda e: e.reciprocal(out=hst_[:, 0:8], in_=hst_[:, 0:8]), [hst_[:, 0:8]], [hst_[:, 0:8]])
                tt(ON[:], Osb_h[:], hst_[:, 0:8].unsqueeze(2).to_broadcast([64, NCHK, 128]), ALU.mult)
                pd16 = pD[:, :].bitcast(BF16)
                for j in range(NCHK):
                    tr(pd16[:, j * 64:(j + 1) * 64], ON[:, j, :], identb[0:64, 0:64])
                act(hsil[:], hgx[:], AF.Silu)
                stt(ho[:], pd16[:, 0:T], pv[:, HNG + hd:HNG + hd + 1], hsil[:], ALU.mult, ALU.mult)
                P.dma(mixT[1536 + hd * 128:1536 + (hd + 1) * 128, t0:t0 + T], ho[:], queue="sp")
    P.finish()
    return nc, P


TB = 384
NST = 3
NEG = -1.0e30


def build_B(NTB, final=False, simw=False, n_groups=32):
    nc = bass.Bass("TRN2", target_bir_lowering=False)
    NTOK = NTB * TB
    dt = nc.dram_tensor
    hT = dt("hT", [D, NTOK], F32, kind="ExternalInput").ap()
    mixT = dt("mixT", [D, NTOK], BF16, kind="ExternalInput").ap()
    WoC = dt("WoC", [32, 128, 32, 128], F32, kind="ExternalInput").ap()
    WqC = dt("WqC", [16, 128, 32, 128], F32, kind="ExternalInput").ap()
    UTc = dt("UTc", [128, 128, 32, 128], F32, kind="ExternalInput").ap()
    Vr = dt("Vr", [128, 128, 4096], F32, kind="ExternalInput").ap()
    outT = dt("outT", [D, NTOK], F32, kind="ExternalOutput").ap()

    P = Prog(nc)
    sb, ps = P.sb, P.ps

    def V(fn, reads, writes):
        return P.op("dve", fn, reads=reads, writes=writes)

    def A(fn, reads, writes):
        return P.op("act", fn, reads=reads, writes=writes)

    def mm(out, lhsT, rhs, start=True, stop=True):
        return P.op("pe", lambda e: e.matmul(out, lhsT=lhsT, rhs=rhs, start=start, stop=stop),
                    reads=[lhsT, rhs], writes=[out])

    def tr(out, in_, ident):
        return P.op("pe", lambda e: e.transpose(out=out, in_=in_, identity=ident), reads=[in_, ident], writes=[out])

    def tt(out, in0, in1, op):
        return V(lambda e: e.tensor_tensor(out=out, in0=in0, in1=in1, op=op), [in0, in1], [out])

    def ts(out, in0, s1, op0, s2=None, op1=None):
        rd = [in0] + [s for s in (s1, s2) if not isinstance(s, (int, float, type(None)))]
        if op1 is None:
            return V(lambda e: e.tensor_scalar(out=out, in0=in0, scalar1=s1, scalar2=None, op0=op0), rd, [out])
        return V(lambda e: e.tensor_scalar(out=out, in0=in0, scalar1=s1, scalar2=s2, op0=op0, op1=op1), rd, [out])

    def stt(out, in0, s, in1, op0, op1):
        rd = [in0, in1] + ([] if isinstance(s, (int, float)) else [s])
        return V(lambda e: e.scalar_tensor_tensor(out=out, in0=in0, scalar=s, in1=in1, op0=op0, op1=op1), rd, [out])

    def act(out, in_, func, bias=None, scale=None, accum_out=None):
        rd = [in_] + ([] if isinstance(bias, (int, float, type(None))) else [bias])
        kw = {}
        if bias is not None:
            kw["bias"] = bias
        if scale is not None:
            kw["scale"] = scale
        wr = [out]
        if accum_out is not None:
            kw["accum_out"] = accum_out
            wr.append(accum_out)
        return A(lambda e: e.activation(out=out, in_=in_, func=func, **kw), rd, wr)

    def vcopy(out, in_):
        return V(lambda e: e.tensor_copy(out=out, in_=in_), [in_], [out])

    def acopy(out, in_):
        return A(lambda e: e.copy(out=out, in_=in_), [in_], [out])

    def ld(name, shape):
        d_ = dt(name, list(shape), F32, kind="ExternalInput").ap()
        t_ = sb(name + "_s", list(shape), F32)
        P.dma(t_[:], d_)
        return t_

    gf = ld("gain_ffn", [128, 32])
    gfin = ld("gain_fin", [128, 32])
    ceps = ld("c_eps", [128, 2])
    finflag = ld("fin_flag", [128, 1])
    tokmask_d = dt("tokmask", [128, NTOK], F32, kind="ExternalInput").ap()
    tokmask = sb("tokmask_s", [128, TB], F32)
    keysT_d = dt("keysT", [128, 16, 128], F32, kind="ExternalInput").ap()
    kst = [sb("kst%d" % i, [128, 128], F32) for i in range(2)]
    identf = ld("c_ident", [128, 128])
    identb = sb("identb", [128, 128], BF16)
    vcopy(identb[:], identf[:])
    onesb = sb("onesb", [128, 128], BF16)
    V(lambda e: e.memset(onesb[:], 1.0), [], [onesb[:]])

    h1 = sb("h1", [128, 32, TB], F32)
    zf = sb("zf", [128, 32, TB], BF16)
    NWB = 2 if simw else 3
    wb = [sb("wb%d" % i, [128, 32, 128], BF16) for i in range(NWB)]
    wst_ = sb("wst_", [128, 16, 128], F32) if simw else None
    NVB = 5
    vb = [sb("vb%d" % i, [128, 4096], BF16) for i in range(NVB)]
    vst_ = sb("vst_", [128, 2048], F32) if simw else None
    hst = [sb("hst%d" % i, [128, TB], F32) for i in range(2)]
    sqb = [sb("sqb%d" % i, [128, TB], BF16) for i in range(2)]
    rstd = sb("rstd", [128, TB], F32)
    qf = [sb("qf%d" % i, [128, TB], F32) for i in range(2)]
    S1 = [sb("S1_%d" % i, [128, 8, 128], F32) for i in range(NST)]
    S2 = [sb("S2_%d" % i, [128, 8, 128], F32) for i in range(NST)]
    scs = [sb("scs_%d" % i, [128, 16], F32) for i in range(NST)]
    S1g = [sb("S1g_%d" % i, [128, 8, 4], F32) for i in range(NST)]
    THg = [sb("THg_%d" % i, [128, 8, 4], F32) for i in range(NST)]
    stmp = sb("stmp", [128, 576], F32)
    top24 = sb("top24", [128, 16, 24], F32)
    cand = sb("cand", [128, 576], F32)
    best = sb("best", [128, 8, 24], F32)
    sc = sb("sc", [128, 64], F32)
    ejunk = sb("ejunk", [128, 16], F32)
    Gh = [sb("Gh%d" % i, [128, 4, 128], BF16) for i in range(2)]
    Eb = [sb("Eb%d" % i, [128, 128], BF16) for i in range(2)]
    Gacc = [sb("Gacc%d" % i, [128, 4, 128], BF16) for i in range(NST)]
    gel = [sb("gel%d" % i, [128, TB], BF16) for i in range(2)]
    GH = [sb("GH%d" % i, [128, 4, TB], BF16) for i in range(1)]

    pj = [ps("pj%d" % i, [128, 512], F32) for i in range(2)]
    pA = ps("pA", [128, 512], F32)
    pG = [ps("pG%d" % i, [128, 512], F32) for i in range(2)]
    pO = [ps("pO%d" % i, [128, 512], F32) for i in range(2)]
    pS = ps("pS", [128, 512], F32)

    wseq = []
    vseq = []

    def wload(slot, src):
        if simw:
            for hf_ in range(2):
                P.dma(wst_[:], src[:, 16 * hf_:16 * hf_ + 16, :], queue="sp")
                vcopy(wb[slot][:, 16 * hf_:16 * hf_ + 16, :], wst_[:])
        else:
            P.dma(wb[slot][:], src, queue="pool", max_dma_last_dim=8192)

    def vload(slot, src):
        if simw:
            for hf_ in range(2):
                P.dma(vst_[:], src[:, 2048 * hf_:2048 * hf_ + 2048], queue="sp")
                vcopy(vb[slot][:, 2048 * hf_:2048 * hf_ + 2048], vst_[:])
        else:
            P.dma(vb[slot][:], src, queue="pool", max_dma_last_dim=8192)

    for ti in range(NTB):
        for dc in range(32):
            wseq.append(WoC[dc])
        for c in range(16):
            wseq.append(WqC[c])
        for g in range(n_groups):
            for il in range(4):
                wseq.append(UTc[g * 4 + il])
                vseq.append(Vr[g * 4 + il])
    wst = {"i": 0, "u": 0}
    vst = {"i": 0, "u": 0}

    def wnext():
        u = wst["u"]
        while wst["i"] < min(u + NWB, len(wseq)):
            wload(wst["i"] % NWB, wseq[wst["i"]])
            wst["i"] += 1
        wst["u"] += 1
        return wb[u % NWB]

    def vprefetch(limit):
        while vst["i"] < min(limit, len(vseq)):
            vload(vst["i"] % NVB, vseq[vst["i"]])
            vst["i"] += 1

    def vnext():
        u = vst["u"]
        assert u < vst["i"]
        vst["u"] += 1
        return vb[u % NVB]

    for ti in range(NTB):
        t0 = ti * TB
        ts_ = slice(t0, t0 + TB)
        for kq in range(4):
            P.dma(zf[:, 8 * kq:8 * kq + 8, :], mixT[1024 * kq:1024 * kq + 1024, ts_].rearrange("(k p) t -> p k t", p=128))
        for dc in range(32):
            w = wnext()
            p = pj[dc % 2]
            for k in range(32):
                mm(p[:, 0:TB], w[:, k, :], zf[:, k, :], start=(k == 0), stop=(k == 31))
            P.dma(hst[dc % 2][:], hT[dc * 128:(dc + 1) * 128, ts_])
            tt(h1[:, dc, :], p[:, 0:TB], hst[dc % 2][:], ALU.add)
        for k in range(32):
            act(sqb[k % 2][:], h1[:, k, :], AF.Square)
            mm(pA[:, 0:TB], onesb[:], sqb[k % 2][:], start=(k == 0), stop=(k == 31))
        act(rstd[:], pA[:, 0:TB], AF.Sqrt, bias=ceps[:, 0:1], scale=1.0 / D)
        V(lambda e: e.reciprocal(out=rstd[:], in_=rstd[:]), [rstd[:]], [rstd[:]])
        for k in range(32):
            stt(zf[:, k, :], h1[:, k, :], gf[:, k:k + 1], rstd[:], ALU.mult, ALU.mult)
        for c in range(16):
            w = wnext()
            p = pj[c % 2]
            for k in range(32):
                mm(p[:, 0:TB], w[:, k, :], zf[:, k, :], start=(k == 0), stop=(k == 31))
            q_ = qf[c % 2]
            acopy(q_[:], p[:, 0:TB])
            P.dma(kst[c % 2][:], keysT_d[:, c, :])
            pp = pG[c % 2]
            for st in range(NST):
                mm(pp[:, st * 128:(st + 1) * 128], q_[:, st * 128:(st + 1) * 128], kst[c % 2][:])
            for st in range(NST):
                dst = (S1, S2)[c % 2][st][:, c // 2, :]
                acopy(dst, pp[:, st * 128:(st + 1) * 128])
        for st in range(NST):
            for c in range(16):
                src = (S1, S2)[c % 2][st][:, c // 2, :]
                V(lambda e, c=c, src=src: e.max(out=top24[:, c, 0:8], in_=src), [src], [top24[:, c, 0:8]])
                V(lambda e, c=c, src=src: e.match_replace(out=stmp[:, 0:128], in_to_replace=top24[:, c, 0:8], in_values=src, imm_value=NEG),
                  [src, top24[:, c, 0:8]], [stmp[:, 0:128]])
                V(lambda e, c=c: e.max(out=top24[:, c, 8:16], in_=stmp[:, 0:128]), [stmp[:, 0:128]], [top24[:, c, 8:16]])
                V(lambda e, c=c: e.match_replace(out=stmp[:, 0:128], in_to_replace=top24[:, c, 8:16], in_values=stmp[:, 0:128], imm_value=NEG),
                  [stmp[:, 0:128], top24[:, c, 8:16]], [stmp[:, 0:128]])
                V(lambda e, c=c: e.max(out=top24[:, c, 16:24], in_=stmp[:, 0:128]), [stmp[:, 0:128]], [top24[:, c, 16:24]])
            for h in range(8):
                c3 = cand[:].rearrange("p (a b) -> p a b", a=24)
                tt(c3, top24[:, 2 * h, :].unsqueeze(2).to_broadcast([128, 24, 24]),
                   top24[:, 2 * h + 1, :].unsqueeze(1).to_broadcast([128, 24, 24]), ALU.add)
                V(lambda e, h=h: e.max(out=best[:, h, 0:8], in_=cand[:]), [cand[:]], [best[:, h, 0:8]])
                V(lambda e, h=h: e.match_replace(out=stmp[:], in_to_replace=best[:, h, 0:8], in_values=cand[:], imm_value=NEG),
                  [cand[:], best[:, h, 0:8]], [stmp[:]])
                V(lambda e, h=h: e.max(out=best[:, h, 8:16], in_=stmp[:]), [stmp[:]], [best[:, h, 8:16]])
                V(lambda e, h=h: e.match_replace(out=stmp[:], in_to_replace=best[:, h, 8:16], in_values=stmp[:], imm_value=NEG),
                  [stmp[:], best[:, h, 8:16]], [stmp[:]])
                V(lambda e, h=h: e.max(out=best[:, h, 16:24], in_=stmp[:]), [stmp[:]], [best[:, h, 16:24]])
            ts(sc[:, 0:8], best[:, :, 0], -1.0, ALU.mult)
            for h in range(8):
                act(ejunk[:, 0:16], best[:, h, 0:16], AF.Exp, bias=sc[:, h:h + 1], accum_out=sc[:, 8 + h:9 + h])
            act(sc[:, 8:16], sc[:, 8:16], AF.Ln)
            tt(sc[:, 24:32], sc[:, 0:8], sc[:, 8:16], ALU.subtract)
            tt(sc[:, 16:24], best[:, :, 15], best[:, :, 16], ALU.add)
            ts(sc[:, 16:24], sc[:, 16:24], 0.5, ALU.mult)
            vcopy(scs[st][:, 0:8], sc[:, 24:32])
            vcopy(scs[st][:, 8:16], sc[:, 16:24])
        for g in range(n_groups):
            gh = GH[0]
            vprefetch(vst["u"] + NVB)
            for st in range(NST):
                gs = slice(g * 4, g * 4 + 4)
                tt(S1g[st][:], S1[st][:, :, gs], scs[st][:, 0:8].unsqueeze(2).to_broadcast([128, 8, 4]), ALU.add)
                tt(THg[st][:], scs[st][:, 8:16].unsqueeze(2).to_broadcast([128, 8, 4]), S1[st][:, :, gs], ALU.subtract)
                for h in range(8):
                    s2 = S2[st][:, h, :]
                    gbuf = Gacc[st] if h == 0 else Gh[h % 2]
                    for il in range(4):
                        i = g * 4 + il
                        eb = Eb[(h * 4 + il) % 2]
                        act(eb[:], s2, AF.Exp, bias=S1g[st][:, h, il:il + 1])
                        stt(gbuf[:, il, :], s2, THg[st][:, h, il:il + 1], eb[:], ALU.is_ge, ALU.mult)
                    if h > 0:
                        tt(Gacc[st][:], Gacc[st][:], gbuf[:], ALU.add)
            vtiles = []
            for il in range(4):
                i = g * 4 + il
                w = wnext()
                p = pj[i % 2]
                for k in range(32):
                    mm(p[:, 0:TB], w[:, k, :], zf[:, k, :], start=(k == 0), stop=(k == 31))
                ge = gel[i % 2]
                act(ge[:], p[:, 0:TB], AF.Gelu)
                pg16 = pG[i % 2][:, :].bitcast(BF16)
                for st in range(NST):
                    tr(pg16[:, st * 128:(st + 1) * 128], Gacc[st][:, il, :], identb[:])
                tt(gh[:, il, :], ge[:], pg16[:, 0:TB], ALU.mult)
                vtiles.append(vnext())
            for dc in range(32):
                po = pO[dc % 2]
                for il in range(4):
                    mm(po[:, 0:TB], vtiles[il][:, dc * 128:(dc + 1) * 128], gh[:, il, :], start=(il == 0), stop=(il == 3))
                tt(h1[:, dc, :], h1[:, dc, :], po[:, 0:TB], ALU.add)
        P.dma(tokmask[:], tokmask_d[:, ts_])
        for k in range(32):
            act(sqb[k % 2][:], h1[:, k, :], AF.Square)
            mm(pA[:, 0:TB], onesb[:], sqb[k % 2][:], start=(k == 0), stop=(k == 31))
        act(rstd[:], pA[:, 0:TB], AF.Sqrt, bias=ceps[:, 0:1], scale=1.0 / D)
        V(lambda e: e.reciprocal(out=rstd[:], in_=rstd[:]), [rstd[:]], [rstd[:]])
        for k in range(32):
            tmp_ = hst[k % 2]
            stt(tmp_[:], h1[:, k, :], gfin[:, k:k + 1], rstd[:], ALU.mult, ALU.mult)
            tt(tmp_[:], tmp_[:], h1[:, k, :], ALU.subtract)
            stt(h1[:, k, :], tmp_[:], finflag[:, 0:1], h1[:, k, :], ALU.mult, ALU.add)
            tt(h1[:, k, :], h1[:, k, :], tokmask[:], ALU.mult)
        for kq in range(4):
            P.dma(outT[1024 * kq:1024 * kq + 1024, ts_].rearrange("(k p) t -> p k t", p=128), h1[:, 8 * kq:8 * kq + 8, :], queue="sp")
    P.finish()
    return nc, P


def chunk_cols(hh):
    ch = []
    ar = np.arange
    ch.append(ar(4608, 4736))
    ch.append(ar(4736, 4864))
    for hp in range(6):
        o = hh * 768 + hp * 128
        ch.append(o + ar(128))
        ch.append(1536 + o + ar(128))
        ch.append(3072 + o + ar(128))
    kvs = [0, 0, 1] if hh == 0 else [1, 2, 2]
    for s in range(3):
        kc = 6400 + kvs[s] * 64 + ar(64)
        ch.append(np.concatenate([kc, kc]))
    vA = 6592 + kvs[0] * 64 + ar(64)
    vB = 6592 + kvs[1] * 64 + ar(64)
    vC = 6592 + kvs[2] * 64 + ar(64)
    ch.append(np.concatenate([vA, vB]))
    ch.append(np.concatenate([vC, vC]))
    for qc in range(6):
        ch.append(4864 + hh * 768 + qc * 128 + ar(128))
    for hd in range(4):
        o = (4 * hh + hd) * 128
        for base in (6784, 7808, 8832, 9856):
            ch.append(base + o + ar(128))
    assert len(ch) == 47
    return ch


def consts_A():
    c = {}
    c["c_ident"] = np.eye(128, dtype=np.float32)
    p = np.arange(128)
    c["c_blk"] = (p[:, None] // 64 == p[None, :] // 64).astype(np.float32)
    pit = np.zeros((128, 128), np.float32)
    for m in range(128):
        j = m % 64
        if j < 8:
            pit[m + 8, m] = 1.0
        elif j < 16:
            pit[m - 8, m] = 1.0
    c["c_pit"] = pit
    rm = np.ones((128, 512), np.float32)
    rm[:, ::64] = 0.0
    c["c_rmask"] = rm
    i = np.arange(64)
    su = (i[:, None] < i[None, :]).astype(np.float32)
    sl = (i[:, None] > i[None, :]).astype(np.float32)
    ui = (i[:, None] <= i[None, :]).astype(np.float32)
    idm = np.eye(64, dtype=np.float32)
    tri = np.stack([su, sl, ui, idm], axis=1)
    c["c_tri"] = np.ascontiguousarray(tri)
    j = np.arange(128)
    mprev = (j[:, None] > j[None, :]).astype(np.float32)
    mcur = (j[:, None] <= j[None, :]).astype(np.float32)
    c["c_amask"] = np.ascontiguousarray(np.stack([mprev, mcur], axis=1))
    mm0 = np.zeros((16, 128), np.float32)
    for m in range(16):
        mm0[m, PADF + m:] = 1.0
    c["c_mmeta0"] = mm0
    c["c_pos"] = np.tile(np.arange(512, dtype=np.float32)[None, :], (128, 1))
    eps = np.zeros((128, 2), np.float32)
    eps[:, 0] = 1e-5
    eps[:, 1] = 64e-5
    c["c_eps"] = eps
    return c


def prep_A(inp, l, hh, hT_pad):
    m = dict(consts_A())
    if hT_pad is not None:
        m["hT"] = np.ascontiguousarray(hT_pad, dtype=np.float32)
    m["gain"] = np.ascontiguousarray(inp["norm_mix"][l].reshape(32, 128).T)
    ch = chunk_cols(hh)
    W = inp["w_in"][l]
    Wc = np.empty((47, 128, 32, 128), np.float32)
    for c, cols in enumerate(ch):
        Wc[c] = W[:, cols].reshape(32, 128, 128).transpose(1, 0, 2)
    m["Wc"] = Wc
    pv = np.zeros((128, 80), np.float32)
    mu = inp["rwkv_mu"][l]
    for c in range(20):
        pv[:, c] = mu[ch[c]]
    own = hh * 768 + np.arange(768)
    for base, name in ((20, "rwkv_w0"), (26, "rwkv_a0"), (32, "rwkv_k_k"), (38, "rwkv_k_a"), (44, "rwkv_r_k"),
                       (50, "rwkv_lnx_w"), (56, "rwkv_lnx_b")):
        pv[:, base:base + 6] = inp[name][l][own].reshape(6, 128).T
    hown = hh * 512 + np.arange(512)
    pv[:, 62:66] = inp["hgrn_lb"][0][hown].reshape(4, 128).T
    pv[:, 66:70] = inp["hgrn_lb"][l][hown].reshape(4, 128).T
    pv[:, 70:74] = inp["hgrn_norm"][l][hown].reshape(4, 128).T
    pv[:, 74] = 1.0 if l > 0 else 0.0
    p = np.arange(128) % 64
    pv[:, 75] = np.where(p < 16, p % 8, 0)
    pv[:, 76] = (p < 16).astype(np.float32)
    pv[:, 77] = np.where(p < 8, -1.0, np.where(p < 16, 1.0, 0.0))
    m["pv"] = pv
    lw = np.empty((128, 768), np.float32)
    lw[:64] = inp["rwkv_w2"][l][:, own]
    lw[64:] = inp["rwkv_a2"][l][:, own]
    m["lw"] = lw
    m["g2"] = np.ascontiguousarray(inp["rwkv_g2"][l][:, own])
    m["sinks"] = np.tile(inp["swa_sinks"][l][12 * hh:12 * hh + 12][None, :], (128, 1)).astype(np.float32)
    return m


def pad_hT(h, ntok):
    out = np.zeros((4096, ntok), np.float32)
    L = min(h.shape[0], ntok - PADF)
    out[:, PADF:PADF + L] = h[:L].T
    return out


def mix_rows(hh):
    r = np.concatenate([hh * 768 + np.arange(768), 1536 + hh * 768 + np.arange(768), 3072 + hh * 512 + np.arange(512)])
    return r


def chunkmajor(W):
    N = W.shape[1]
    return np.ascontiguousarray(W.reshape(32, 128, N // 128, 128).transpose(2, 1, 0, 3))

def prep_B_weights(inp, l):
    m = {}
    m["WoC"] = chunkmajor(inp["w_out"][l])
    m["WqC"] = chunkmajor(inp["peer_wq"][l])
    U = inp["peer_u"][l]
    m["UTc"] = np.ascontiguousarray(U.reshape(128, 128, 32, 128).transpose(0, 3, 2, 1))
    m["Vr"] = np.ascontiguousarray(inp["peer_v"][l].reshape(128, 128, 4096))
    keys = inp["peer_keys"][l]
    m["keysT"] = np.ascontiguousarray(keys.reshape(16, 128, 128).transpose(2, 0, 1))
    m["gain_ffn"] = np.ascontiguousarray(inp["norm_ffn"][l].reshape(32, 128).T)
    m["gain_fin"] = np.ascontiguousarray(inp["norm_final"].reshape(32, 128).T)
    eps = np.zeros((128, 2), np.float32); eps[:, 0] = 1e-5; eps[:, 1] = 64e-5
    m["c_eps"] = eps
    m["c_ident"] = np.eye(128, dtype=np.float32)
    return m


NT_A = 9
NTOK = NT_A * T
NTB_B = NTOK // 2 // TB
_PROGS = {}


def _prog(kind):
    if kind not in _PROGS:
        if kind == "A":
            _PROGS[kind] = build_A(NT_A)[0]
        else:
            _PROGS[kind] = build_B(NTB_B)[0]
    return _PROGS[kind]


def kernel(**inputs):
    inp = {k: np.asarray(v) for k, v in inputs.items()}
    x = inp["x"]
    B = x.shape[0]
    n_cores = 8
    assert B * 2 == n_cores
    meta = inp["meta_tokens"].astype(np.float32)
    hT = [pad_hT(np.concatenate([meta, x[b]], axis=0), NTOK) for b in range(B)]
    half = NTOK // 2
    depth = inp["w_in"].shape[0]
    for l in range(depth):
        wmaps = [prep_A(inp, l, hh, None) for hh in range(2)]
        in_maps = []
        for c in range(n_cores):
            m = dict(wmaps[c % 2])
            m["hT"] = hT[c // 2]
            in_maps.append(m)
        res = run_bass_kernel_spmd(_prog("A"), in_maps, core_ids=list(range(n_cores)))
        del in_maps, wmaps
        mixT = []
        for b in range(B):
            full = np.empty((4096, NTOK), ml_dtypes.bfloat16)
            for hh in range(2):
                full[mix_rows(hh)] = np.asarray(res.results[2 * b + hh]["mixT"])
            mixT.append(full)
        del res
        wm = prep_B_weights(inp, l)
        wm["fin_flag"] = np.full((128, 1), 1.0 if l == depth - 1 else 0.0, np.float32)
        in_maps = []
        for c in range(n_cores):
            b, hf = c // 2, c % 2
            m = dict(wm)
            m["hT"] = np.ascontiguousarray(hT[b][:, hf * half:(hf + 1) * half])
            tm = np.zeros((128, NTOK), np.float32)
            tm[:, PADF:PADF + meta.shape[0] + x.shape[1]] = 1.0
            m["tokmask"] = np.ascontiguousarray(tm[:, hf * half:(hf + 1) * half])
            m["mixT"] = np.ascontiguousarray(mixT[b][:, hf * half:(hf + 1) * half])
            in_maps.append(m)
        res = run_bass_kernel_spmd(_prog("B"), in_maps, core_ids=list(range(n_cores)))
        del in_maps, wm, mixT
        for c in range(n_cores):
            b, hf = c // 2, c % 2
            hT[b][:, hf * half:(hf + 1) * half] = np.asarray(res.results[c]["outT"])
        del res
    out = np.empty((B, x.shape[1], 4096), np.float32)
    o0 = PADF + meta.shape[0]
    for b in range(B):
        out[b] = hT[b][:, o0:o0 + x.shape[1]].T
    return out
```
